# Optimizing a Trainium2 kernel written in Bass

```python
import jax, jax.numpy as jnp
from jax import lax
import numpy as np

D_MODEL = 2048
BATCH = 16
SEQ = 2048
DEPTH = 4

MEM_LEN = 256
EPS = 1e-6

ATTN_HEAD_DIM = 64
ATTN_Q_HEADS = D_MODEL // 128
ATTN_KV_HEADS = ATTN_Q_HEADS // 4
ATTN_Q_DIM = ATTN_Q_HEADS * ATTN_HEAD_DIM
ATTN_KV_DIM = ATTN_KV_HEADS * ATTN_HEAD_DIM
WINDOW = 128

SGU_CHUNK = 128
SGU_GROUP_DIM = 128
SGU_GROUPS = D_MODEL // 256
SGU_WIDTH = SGU_GROUPS * SGU_GROUP_DIM

HGRN_HEAD_DIM = 128
HGRN_HEADS = D_MODEL // 256
HGRN_WIDTH = HGRN_HEADS * HGRN_HEAD_DIM
HGRN_CHUNK = 64

X_HEADS = 4
X_HEAD_DIM = 128
X_DIM = X_HEADS * X_HEAD_DIM

D_FF = -(-8 * D_MODEL // (3 * 256)) * 256

IN_SIZES = (ATTN_Q_DIM, ATTN_KV_DIM, ATTN_KV_DIM, SGU_WIDTH, SGU_WIDTH,
            HGRN_WIDTH, HGRN_WIDTH, HGRN_WIDTH, HGRN_WIDTH)
IN_WIDTH = sum(IN_SIZES)
N_BRANCH = 3

kernel_name = 'hybrid_swa_sgu_hgrn2_gated_trunk'


def _split_points():
    pts, acc = [], 0
    for s in IN_SIZES[:-1]:
        acc += s
        pts.append(acc)
    return pts


def rms_norm(x, g):
    xf = x.astype(jnp.float32)
    y = xf * lax.rsqrt(jnp.mean(xf * xf, axis=-1, keepdims=True) + EPS)
    return (y * g.astype(jnp.float32)).astype(x.dtype)


def layer_norm(x, g, b):
    xf = x.astype(jnp.float32)
    mu = jnp.mean(xf, axis=-1, keepdims=True)
    xc = xf - mu
    y = xc * lax.rsqrt(jnp.mean(xc * xc, axis=-1, keepdims=True) + EPS)
    return (y * g.astype(jnp.float32) + b.astype(jnp.float32)).astype(x.dtype)


def sliding_window_attention(q, k, v, sinks):
    B, S = q.shape[0], q.shape[1]
    nb = S // WINDOW
    G = ATTN_Q_HEADS // ATTN_KV_HEADS
    qb = q.reshape(B, nb, WINDOW, ATTN_KV_HEADS, G, ATTN_HEAD_DIM)

    def band(t):
        tb = t.reshape(B, nb, WINDOW, ATTN_KV_HEADS, ATTN_HEAD_DIM)
        prev = jnp.pad(tb, ((0, 0), (1, 0), (0, 0), (0, 0), (0, 0)))[:, :-1]
        return jnp.concatenate([prev, tb], axis=2)

    kb, vb = band(k), band(v)
    s = jnp.einsum('bnqkgd,bnskd->bnkgqs', qb, kb,
                   preferred_element_type=jnp.float32) * (ATTN_HEAD_DIM ** -0.5)
    t_idx = jnp.arange(WINDOW)[:, None] + WINDOW
    s_idx = jnp.arange(2 * WINDOW)[None, :]
    diff = t_idx - s_idx
    blk = jnp.arange(nb)[:, None, None]
    valid = (diff >= 0) & (diff < WINDOW) & (blk * WINDOW + s_idx - WINDOW >= 0)
    s = jnp.where(valid[None, :, None, None, :, :], s, -jnp.inf)
    sink = sinks.astype(jnp.float32).reshape(1, 1, ATTN_KV_HEADS, G, 1, 1)
    m = jnp.maximum(jnp.max(s, axis=-1, keepdims=True), sink)
    p = jnp.exp(s - m)
    p = p / (jnp.sum(p, axis=-1, keepdims=True) + jnp.exp(sink - m))
    o = jnp.einsum('bnkgqs,bnskd->bnqkgd', p.astype(v.dtype), vb)
    return o.reshape(B, S, ATTN_Q_DIM)


def chunked_spatial_gating(u, v, ln_g, ln_b, w_s, b_s):
    B, S = u.shape[0], u.shape[1]
    nc = S // SGU_CHUNK
    vn = layer_norm(v, ln_g, ln_b).reshape(B, nc, SGU_CHUNK, SGU_GROUPS, SGU_GROUP_DIM)
    w = w_s * jnp.tril(jnp.ones((SGU_CHUNK, SGU_CHUNK), dtype=w_s.dtype))[None]
    mixed = jnp.einsum('gts,bcsge->bctge', w, vn) + jnp.transpose(b_s)[None, None, :, :, None]
    return u * mixed.reshape(B, S, SGU_WIDTH)


def hgrn2(f_logit, i_in, q_in, g_out, lb, norm_g):
    B, S = q_in.shape[0], q_in.shape[1]
    H, dk, C = HGRN_HEADS, HGRN_HEAD_DIM, HGRN_CHUNK
    nc = S // C
    f = lb + (1.0 - lb) * jax.nn.sigmoid(f_logit.astype(jnp.float32))
    log_f = jnp.log(f)
    k = 1.0 - f
    qf = jax.nn.silu(q_in.astype(jnp.float32)) * (dk ** -0.5)
    vf = i_in.astype(jnp.float32)

    def to_chunks(t):
        return t.reshape(B, nc, C, H, dk).transpose(1, 0, 3, 2, 4)

    causal = jnp.tril(jnp.ones((C, C), dtype=bool))

    def step(state, inp):
        qc, kc, vc, lfc = inp
        b = jnp.cumsum(lfc, axis=2)
        o_inter = jnp.einsum('bhtd,bhde->bhte', qc * jnp.exp(b), state)
        diff = b[:, :, :, None, :] - b[:, :, None, :, :]
        decay = jnp.exp(jnp.where(causal[None, None, :, :, None], diff, -jnp.inf))
        a = jnp.einsum('bhtd,bhsd,bhtsd->bhts', qc, kc, decay)
        o_intra = jnp.einsum('bhts,bhse->bhte', a, vc)
        b_last = b[:, :, -1:, :]
        state = jnp.exp(b_last[:, :, 0, :])[..., None] * state + \
            jnp.einsum('bhsd,bhse->bhde', kc * jnp.exp(b_last - b), vc)
        return state, o_inter + o_intra

    s0 = jnp.zeros((B, H, dk, dk), jnp.float32)
    _, o = lax.scan(step, s0, (to_chunks(qf), to_chunks(k), to_chunks(vf), to_chunks(log_f)))
    o = o.transpose(1, 0, 3, 2, 4).reshape(B, S, H, dk)
    o = o * lax.rsqrt(jnp.mean(o * o, axis=-1, keepdims=True) + EPS) * norm_g.astype(jnp.float32)
    o = o.reshape(B, S, HGRN_WIDTH) * jax.nn.silu(g_out.astype(jnp.float32))
    return o.astype(q_in.dtype)


def memory_cross_attention(h, mem_n, w_q, w_kv, w_o):
    B, S = h.shape[0], h.shape[1]
    M = mem_n.shape[1]
    q = (h @ w_q).reshape(B, S, X_HEADS, X_HEAD_DIM)
    k, v = jnp.split(mem_n @ w_kv, 2, axis=-1)
    k = k.reshape(B, M, X_HEADS, X_HEAD_DIM)
    v = v.reshape(B, M, X_HEADS, X_HEAD_DIM)
    s = jnp.einsum('bqhd,bmhd->bhqm', q, k, preferred_element_type=jnp.float32) * (X_HEAD_DIM ** -0.5)
    p = jax.nn.softmax(s, axis=-1).astype(v.dtype)
    o = jnp.einsum('bhqm,bmhd->bqhd', p, v).reshape(B, S, X_DIM)
    return o @ w_o


def setup_inputs(seed: int = 0) -> dict:
    key = jax.random.key(seed)
    ks = jax.random.split(key, 32)
    L, D = DEPTH, D_MODEL
    f32 = jnp.float32

    def dense(k, shape, fan_in):
        return jax.random.normal(k, shape, f32) * (fan_in ** -0.5)

    def gain(k, shape):
        return 1.0 + 0.02 * jax.random.normal(k, shape, f32)

    return {
        'x': jax.random.normal(ks[0], (BATCH, SEQ, D), f32),
        'mem': jax.random.normal(ks[1], (BATCH, MEM_LEN, D), f32),
        'norm_mix': gain(ks[2], (L, D)),
        'w_in': dense(ks[3], (L, D, IN_WIDTH), D),
        'w_gate': dense(ks[4], (L, D, N_BRANCH * D), D),
        'sinks': 0.5 * jax.random.normal(ks[5], (L, ATTN_Q_HEADS), f32),
        'sgu_ln_g': gain(ks[6], (L, SGU_WIDTH)),
        'sgu_ln_b': 0.02 * jax.random.normal(ks[7], (L, SGU_WIDTH), f32),
        'sgu_w': dense(ks[8], (L, SGU_GROUPS, SGU_CHUNK, SGU_CHUNK), SGU_CHUNK),
        'sgu_b': 1.0 + 0.02 * jax.random.normal(ks[9], (L, SGU_GROUPS, SGU_CHUNK), f32),
        'hgrn_lb': 0.5 * jax.random.normal(ks[10], (L, HGRN_WIDTH), f32),
        'hgrn_norm': gain(ks[11], (L, HGRN_HEAD_DIM)),
        'w_br_a': dense(ks[12], (L, ATTN_Q_DIM, D), ATTN_Q_DIM),
        'w_br_b': dense(ks[13], (L, SGU_WIDTH, D), SGU_WIDTH),
        'w_br_c': dense(ks[14], (L, HGRN_WIDTH, D), HGRN_WIDTH),
        'w_out': dense(ks[15], (L, D, D), D),
        'norm_x': gain(ks[16], (L, D)),
        'mem_norm': gain(ks[17], (D,)),
        'w_xq': dense(ks[18], (L, D, X_DIM), D),
        'w_xkv': dense(ks[19], (L, D, 2 * X_DIM), D),
        'w_xo': dense(ks[20], (L, X_DIM, D), X_DIM),
        'norm_ffn': gain(ks[21], (L, D)),
        'w_ffn_in': dense(ks[22], (L, D, 2 * D_FF), D),
        'w_ffn_out': dense(ks[23], (L, D_FF, D), D_FF),
        'final_norm': gain(ks[24], (D,)),
    }


def reference(x, mem, norm_mix, w_in, w_gate, sinks, sgu_ln_g, sgu_ln_b, sgu_w, sgu_b,
              hgrn_lb, hgrn_norm, w_br_a, w_br_b, w_br_c, w_out, norm_x, mem_norm,
              w_xq, w_xkv, w_xo, norm_ffn, w_ffn_in, w_ffn_out, final_norm):
    B, S = x.shape[0], x.shape[1]
    sm = jax.nn.softmax(hgrn_lb.astype(jnp.float32), axis=0)
    lb_all = jnp.cumsum(sm, axis=0) - sm[0:1]
    mem_n = rms_norm(mem, mem_norm)
    pts = _split_points()
    for l in range(DEPTH):
        h = rms_norm(x, norm_mix[l])
        qa, ka, va, ub, vb, fc, ic, qc, gc = jnp.split(h @ w_in[l], pts, axis=-1)
        y_a = sliding_window_attention(
            qa.reshape(B, S, ATTN_Q_HEADS, ATTN_HEAD_DIM),
            ka.reshape(B, S, ATTN_KV_HEADS, ATTN_HEAD_DIM),
            va.reshape(B, S, ATTN_KV_HEADS, ATTN_HEAD_DIM), sinks[l])
        y_b = chunked_spatial_gating(jax.nn.gelu(ub), jax.nn.gelu(vb),
                                     sgu_ln_g[l], sgu_ln_b[l], sgu_w[l], sgu_b[l])
        y_c = hgrn2(fc, ic, qc, gc, lb_all[l], hgrn_norm[l])
        gate_a, gate_b, gate_c = jnp.split(jax.nn.sigmoid(h @ w_gate[l]), N_BRANCH, axis=-1)
        merged = gate_a * (y_a @ w_br_a[l]) + gate_b * (y_b @ w_br_b[l]) + gate_c * (y_c @ w_br_c[l])
        x = x + merged @ w_out[l]
        x = x + memory_cross_attention(rms_norm(x, norm_x[l]), mem_n, w_xq[l], w_xkv[l], w_xo[l])
        gt, up = jnp.split(rms_norm(x, norm_ffn[l]) @ w_ffn_in[l], 2, axis=-1)
        x = x + (jax.nn.silu(gt) * up) @ w_ffn_out[l]
    return rms_norm(x, final_norm)
```

```python
import numpy as np
import concourse.bass as bass
import concourse.mybir as mybir
from concourse.bass_utils import run_bass_kernel_spmd

F32 = mybir.dt.float32
BF16 = mybir.dt.bfloat16
AF = mybir.ActivationFunctionType
ALU = mybir.AluOpType

NL = 4
D = 2048
SEQ = 2048
T = 512
NBLK = 4
MEM = 256
DFF = 5632
EPS = 1e-6
NEG = -30000.0
SB_BASE = 16512
SB_LIMIT = 229376
RING = 5
WSLOT = 4096

Q0, K0, V0, U0, VB0, F0, I0, QC0, G0 = 0, 1024, 1280, 1536, 2560, 3584, 4608, 5632, 6656
PL = 57
PC_FINAL = 228
PC_MEMN = 244
PC_LB = 260
NPCOL = 292
C_ID, C_BD, C_CUR, C_MR, C_MR0, C_RST, C_OE, C_OO, C_ADJ = 0, 128, 256, 384, 896, 1408, 1920, 2048, 2176
NCC = 2304


def _isnum(x):
    return isinstance(x, (int, float))


class Eng:
    def __init__(self, name, key):
        self.name = name
        self.key = key
        self.cnt = 0
        self.seen = {}
        self.prog = []


class Page:
    __slots__ = ("w", "r")

    def __init__(self):
        self.w = None
        self.r = {}


class Prog:
    PAGE = 512

    def __init__(self, cfg):
        self.cfg = cfg
        self.nc = bass.Bass("TRN2", target_bir_lowering=False)
        self.pe = Eng("pe", "pe")
        self.act = Eng("act", "act")
        self.dve = Eng("dve", "dve")
        self.sp = Eng("sp", "sp")
        self.pool = Eng("pool", "pool")
        self.pages = {}
        self.tinfo = {}
        self.semcnt = {}
        self.semkeys = ["pe", "act", "dve"]
        self.pbump = SB_BASE

    def salloc(self, name, shape, dtype, off=None):
        ds = 4 if dtype == F32 else 2
        per = int(np.prod(shape[1:]))
        nbytes = per * ds
        if off is None:
            off = self.pbump
            self.pbump += (nbytes + 31) // 32 * 32
        assert off + nbytes <= SB_LIMIT, (name, off, nbytes)
        h = self.nc.alloc_sbuf_tensor_at(name, list(shape), dtype, offset=off)
        self.tinfo[h.name] = ("S", off, ds, per)
        return h

    def palloc(self, i):
        h = self.nc.alloc_psum_tensor("psb%d" % i, [128, 512], F32)
        self.tinfo[h.name] = ("P", i * 2048, 4, 512)
        return h

    def _pages(self, ap):
        info = self.tinfo.get(ap.tensor.name)
        if info is None:
            return ()
        space, base, ds, per = info
        dims = ap.ap
        off = ap.offset % per
        span = 1
        for st, cnt in dims[1:]:
            span += abs(st) * (cnt - 1)
        lo = base + off * ds
        hi = lo + span * ds
        res = []
        for pg in range(lo // self.PAGE, (hi - 1) // self.PAGE + 1):
            k = (space, pg)
            p = self.pages.get(k)
            if p is None:
                p = Page()
                self.pages[k] = p
            res.append(p)
        return res

    def newsem(self, key):
        self.semkeys.append(key)
        self.semcnt[key] = 0

    def emit(self, eng, fn, reads, writes, sem=None, ninc=1, extra=()):
        deps = {}

        def add(sv):
            if sv is not None:
                k, v = sv
                if deps.get(k, 0) < v:
                    deps[k] = v

        rp = []
        wp = []
        for ap in reads:
            for p in self._pages(ap):
                rp.append(p)
                add(p.w)
        for ap in writes:
            for p in self._pages(ap):
                wp.append(p)
                add(p.w)
                for kv in p.r.items():
                    add(kv)
        for sv in extra:
            add(sv)
        waits = []
        for k, v in deps.items():
            if k == eng.key and eng.name == "pe":
                continue
            if eng.seen.get(k, 0) < v:
                waits.append((k, v))
                eng.seen[k] = v
        if sem is None:
            eng.cnt += 1
            ev = (eng.key, eng.cnt)
        else:
            self.semcnt[sem] += 16 * ninc
            ev = (sem, self.semcnt[sem])
        eng.prog.append((waits, fn, ev))
        for p in rp:
            if p.r.get(ev[0], 0) < ev[1]:
                p.r[ev[0]] = ev[1]
        for p in wp:
            p.w = ev
            p.r = {}
        return ev

    def a_act(self, out, in_, func, scale=1.0, bias=None, accum=None):
        reads = [in_]
        if not _isnum(scale):
            reads.append(scale)
        kw = {"scale": scale}
        if bias is not None:
            kw["bias"] = bias
            if not _isnum(bias):
                reads.append(bias)
        writes = [out]
        if accum is not None:
            kw["accum_out"] = accum
            writes.append(accum)
        self.emit(self.act, lambda h: h.activation(out, in_, func, **kw), reads, writes)

    def v_tt(self, out, in0, in1, op):
        self.emit(self.dve, lambda h: h.tensor_tensor(out, in0, in1, op), [in0, in1], [out])

    def v_ts(self, out, in0, s1, s2, op0, op1=None):
        reads = [in0] + [s for s in (s1, s2) if s is not None and not _isnum(s)]
        if op1 is None:
            self.emit(self.dve, lambda h: h.tensor_scalar(out, in0, s1, None, op0), reads, [out])
        else:
            self.emit(self.dve, lambda h: h.tensor_scalar(out, in0, s1, s2, op0, op1), reads, [out])

    def v_stt(self, out, in0, scalar, in1, op0, op1):
        reads = [in0, in1] + ([] if _isnum(scalar) else [scalar])
        self.emit(self.dve, lambda h: h.scalar_tensor_tensor(out, in0, scalar, in1, op0, op1), reads, [out])

    def v_copy(self, out, in_):
        self.emit(self.dve, lambda h: h.tensor_copy(out, in_), [in_], [out])

    def v_memset(self, ap, val):
        self.emit(self.dve, lambda h: h.memset(ap, val), [], [ap])

    def v_scan(self, out, d0, d1, init, op0, op1):
        self.emit(self.dve, lambda h: h.tensor_tensor_scan(out, d0, d1, init, op0, op1), [d0, d1], [out])

    def v_recip(self, out, in_):
        self.emit(self.dve, lambda h: h.reciprocal(out, in_), [in_], [out])

    def v_bnstats(self, out, in_):
        self.emit(self.dve, lambda h: h.bn_stats(out, in_), [in_], [out])

    def v_bnaggr(self, out, in_):
        self.emit(self.dve, lambda h: h.bn_aggr(out, in_), [in_], [out])

    def v_reduce(self, out, in_, op, axis):
        self.emit(self.dve, lambda h: h.tensor_reduce(out, in_, op, axis), [in_], [out])

    def mm(self, out, pairs, start=True):
        n = len(pairs)

        def fn(h):
            ins = None
            for i, (l, r) in enumerate(pairs):
                ins = h.matmul(out, l, r, start=(start and i == 0), stop=(i == n - 1))
            return ins

        reads = []
        for l, r in pairs:
            reads.append(l)
            reads.append(r)
        self.emit(self.pe, fn, reads, [out])

    def tr(self, out, in_, ident):
        self.emit(self.pe, lambda h: h.transpose(out, in_, ident), [in_, ident], [out])

    def dma(self, eng, pieces, sem, reads=(), writes=(), extra=()):
        def fn(h):
            return [h.dma_start(out=o, in_=i) for o, i in pieces]

        return self.emit(eng, fn, list(reads), list(writes), sem=sem, ninc=len(pieces), extra=extra)

    def build(self):
        cfg = self.cfg
        nc = self.nc
        NS = cfg["nseq"]
        dr = {}

        def din(name, shape):
            dr[name] = nc.dram_tensor(name, list(shape), F32, kind="ExternalInput").ap()

        din("x", [NS, SEQ, D])
        din("mem", [NS, MEM, D])
        din("w_in", [NL, D, 7680])
        din("w_gate", [NL, D, 6144])
        din("w_br_a", [NL, 1024, D])
        din("w_br_b", [NL, 1024, D])
        din("w_br_c", [NL, 1024, D])
        din("w_out", [NL, D, D])
        din("w_xq", [NL, D, 512])
        din("w_xkv", [NL, D, 1024])
        din("w_xo", [NL, 512, D])
        din("w_ffn_in", [NL, D, 2 * DFF])
        din("w_ffn_out", [NL, DFF, D])
        din("pcols", [128, NPCOL])
        din("consts", [128, NCC])
        din("sgu_g", [NL, 1024])
        din("sgu_bt", [NL, 1024])
        din("sgu_bs", [NL, 1024])
        din("sgu_wT", [NL, 128, 1024])
        self.dr = dr
        self.y = nc.dram_tensor("y", [NS, SEQ, D], F32, kind="ExternalOutput").ap()
        if cfg.get("dbg"):
            self.dbg = nc.dram_tensor("dbg", [128, 16, T], F32, kind="ExternalOutput").ap()

        self.blocks = [self.layer_blocks(l) for l in range(NL)]
        self.nbl = len(self.blocks[0])
        self.wscr = [nc.dram_tensor("wscr%d" % l, [self.nbl, 128, WSLOT], BF16, kind="Internal").ap() for l in range(NL)]

        A = self.salloc
        self.xT = A("xT", [128, 16, T], F32)
        self.hT = A("hT", [128, 16, T], BF16)
        self.wring = [A("wr%d" % i, [128, WSLOT], BF16) for i in range(RING)]
        self.memnT = A("memnT", [128, 16, MEM], BF16)
        self.Sst = [A("S%d" % l, [128, 8, 128], F32) for l in range(NL)]
        self.kprev = [A("kp%d" % l, [128, 4, 2, 128], BF16) for l in range(NL)]
        self.vprev = [A("vp%d" % l, [128, 4, 2, 128], BF16) for l in range(NL)]
        self.ident_f = A("ident_f", [128, 128], F32)
        self.ident_b = A("ident_b", [128, 128], BF16)
        self.ones_b = A("ones_b", [128, 128], BF16)
        self.onesEO = A("onesEO", [128, 2, 128], BF16)
        self.maskBD = A("maskBD", [128, 128], BF16)
        self.maskADJ = A("maskADJ", [128, 128], BF16)
        self.mcur01 = A("mcur01", [128, 128], F32)
        self.maskrow = A("maskrow", [128, 2, 512], BF16)
        self.reset_f = A("reset_f", [128, 512], F32)
        self.pcols = A("pcols_s", [128, NPCOL], F32)
        self.lbm1 = A("lbm1", [128, NL, 8], F32)
        self.oml = A("oml", [128, NL, 8], F32)
        self.esink = A("esink", [128, NL, 8], F32)
        self.sqb = A("sqb", [128, 2, 512], BF16)
        self.rstd = A("rstd", [128, 512], F32)
        self.sg_gb = A("sg_gb", [128, 2, 1024], F32)
        self.bsb = A("bsb", [128, 8, 128], F32)
        self.WsT = A("WsT", [128, 8, 128], BF16)
        self.epsc = A("epsc", [128, 1], F32)
        self.onec = A("onec", [128, 1], F32)
        self.AB = self.pbump
        self.arena_bytes = SB_LIMIT - self.AB
        self.ps = [self.palloc(i) for i in range(8)]

        for k in ["wr%d" % i for i in range(RING)] + ["cs%d" % i for i in range(8)] + ["xld", "mld", "par", "par2", "st0", "st1", "cst", "dbg"]:
            self.newsem(k)

        self.alloc_arena()
        self.bcur = [0] * NL
        self.cast_ev = {}
        self.loadcount = 0
        self.rr = 0

        self.setup()
        self.emit_casts()
        ntile = cfg["ntile"]
        nl = cfg["nl"]
        for s in range(NS):
            self.mem_prep(s)
            self.reset_carry(nl)
            for ti in range(ntile):
                self.load_x(s, ti)
                for l in range(nl):
                    self.bcur[l] = 0
                    self.mixer(l, ti == 0)
                    self.xattn(l)
                    self.ffn(l)
                    assert self.bcur[l] == self.nbl, (self.bcur[l], self.nbl)
                if cfg.get("dbg") and s == 0 and ti == 0:
                    self.dma(self.sp, [(self.dbg[:, :, :], self.xT[:, :, :])], "dbg", reads=[self.xT[:, :, :]])
                self.final_out(s, ti)
        fin = [(k, self.semcnt[k]) for k in ("st0", "st1", "dbg") if self.semcnt[k] > 0]
        self.sp.prog.append((fin, None, None))
        self.replay()
        return nc

    def layer_blocks(self, l):
        dr = self.dr
        B = []

        def std(name, W, r0, nr, c0, ncols):
            kc = nr // 128
            B.append((name, kc, ncols, [("std", W[l][r0:r0 + nr, c0:c0 + ncols], kc)]))

        w_in = dr["w_in"]
        for j in range(4):
            std("q%d" % j, w_in, 0, D, Q0 + 256 * j, 256)
        for b in range(2):
            B.append(("kd%d" % b, 16, 256, [("kdup", w_in[l][:, K0 + 128 * b + 64 * g:K0 + 128 * b + 64 * g + 64], (g, d)) for d in range(2) for g in range(2)]))
        std("vv", w_in, 0, D, V0, 256)
        for j in range(4):
            std("u%d" % j, w_in, 0, D, U0 + 256 * j, 256)
        for j in range(4):
            std("vb%d" % j, w_in, 0, D, VB0 + 256 * j, 256)
        for j in range(4):
            std("ic%d" % j, w_in, 0, D, I0 + 256 * j, 256)
        for hp in range(4):
            std("fc%d" % hp, w_in, 0, D, F0 + 256 * hp, 256)
            std("qc%d" % hp, w_in, 0, D, QC0 + 256 * hp, 256)
            std("gc%d" % hp, w_in, 0, D, G0 + 256 * hp, 256)
        brs = [dr["w_br_a"], dr["w_br_b"], dr["w_br_c"]]
        for i in range(3):
            for j in range(8):
                std("gt%d_%d" % (i, j), dr["w_gate"], 0, D, i * D + 256 * j, 256)
                std("br%d_%d" % (i, j), brs[i], 0, 1024, 256 * j, 256)
        for j in range(8):
            std("out%d" % j, dr["w_out"], 0, D, 256 * j, 256)
        for j in range(2):
            std("xk%d" % j, dr["w_xkv"], 0, D, 256 * j, 256)
        for j in range(2):
            std("xv%d" % j, dr["w_xkv"], 0, D, 512 + 256 * j, 256)
        for j in range(2):
            std("xq%d" % j, dr["w_xq"], 0, D, 256 * j, 256)
        for j in range(2):
            std("xo%d" % j, dr["w_xo"], 0, 512, 1024 * j, 1024)
        for j in range(22):
            std("fg%d" % j, dr["w_ffn_in"], 0, D, 256 * j, 256)
            std("fu%d" % j, dr["w_ffn_in"], 0, D, DFF + 256 * j, 256)
        for npair in range(8):
            for q in range(4):
                std("fo%d_%d" % (npair, q), dr["w_ffn_out"], q * 1408, 1408, 256 * npair, 256)
        return B

    def emit_casts(self):
        nlay = self.cfg["nl"]
        for l in range(nlay):
            for j, (name, kc, ncols, pieces) in enumerate(self.blocks[l]):
                bid = l * self.nbl + j
                dst = self.wscr[l][j]
                ps = []
                for p in pieces:
                    if p[0] == "std":
                        src = p[1].rearrange("(c p) n -> p c n", p=128)
                        d = dst[:, 0:kc * ncols].rearrange("p (c n) -> p c n", c=kc)
                        ps.append((d, src))
                    else:
                        gg, dd = p[2]
                        src = p[1].rearrange("(c p) e -> p c e", p=128)
                        d = dst[:, 0:4096].rearrange("p (c g d e) -> p c g d e", c=16, g=2, d=2, e=64)[:, :, gg, dd, :]
                        ps.append((d, src))
                sem = "cs%d" % (bid % 8)
                ev = self.dma(self.pool, ps, sem)
                self.cast_ev[bid] = ev

    def wget(self, l, name):
        j = self.bcur[l]
        bname, kc, ncols, _ = self.blocks[l][j]
        assert bname == name, (bname, name)
        self.bcur[l] += 1
        bid = l * self.nbl + j
        slot = self.rr % RING
        self.rr += 1
        dst = self.wring[slot][:, 0:kc * ncols]
        src = self.wscr[l][j][:, 0:kc * ncols]
        self.dma(self.sp, [(dst, src)], "wr%d" % slot, writes=[dst], extra=[self.cast_ev[bid]])
        return self.wring[slot][:, 0:kc * ncols].rearrange("p (c n) -> p c n", c=kc)

    def alloc_arena(self):
        AB = self.AB

        def layout(specs):
            off = AB
            res = {}
            for name, shape, dt in specs:
                h = self.salloc(name, shape, dt, off=off)
                ds = 4 if dt == F32 else 2
                off += (int(np.prod(shape[1:])) * ds + 31) // 32 * 32
                res[name] = h
            assert off <= SB_LIMIT, (off, SB_LIMIT, [s[0] for s in specs])
            return res

        self.a0 = layout([("cf", [128, NCC], F32), ("hl_e", [128, NL, 8], F32), ("hl_s", [128, 8], F32), ("hl_sm", [128, NL, 8], F32)])
        self.a_x = layout([("xtok", [128, 4, D], F32)])
        self.a_m = layout([("memtok", [128, 2, D], F32), ("mjunk", [128, D], BF16), ("mss", [128, 2], F32), ("mrs", [128, 2], F32)])
        pre = [("yaT", [128, 8, T], BF16), ("ybT", [128, 8, T], BF16), ("ycT", [128, 8, T], BF16)]
        self.a_m1 = layout(pre[:1] + [("qT", [128, 8, T], BF16), ("kTpad", [128, 4, 2, T], BF16), ("vaug", [128, 4, 4, 2, 128], BF16),
                                  ("pT", [128, 4, 512], BF16), ("rden", [128, 2, 512], F32)])
        self.a_m2 = layout(pre[:2] + [("uT", [128, 8, T], BF16), ("vn", [128, 4, 1024], BF16), ("vg", [128, 2, 1024], F32),
                                  ("gA", [128, 2, 512], F32), ("gB", [128, 2, 512], F32), ("bnst", [128, 2, 2, 6], F32),
                                  ("mv", [128, 2, 2], F32), ("lrs", [128, 2], F32)])
        self.a_m3 = layout(pre + [("vtok", [128, 4, 1024], BF16), ("hA", [128, 512], F32), ("hB", [128, 512], F32), ("hC", [128, 512], F32),
                                  ("sq", [128, 512], BF16), ("q16", [128, 512], BF16), ("q32", [128, 512], BF16), ("k16", [128, 512], BF16),
                                  ("kf16", [128, 512], BF16), ("kf32", [128, 512], BF16),
                                  ("ketok", [128, 4, 4, 128], BF16), ("a1m", [128, 512], BF16), ("a2m", [128, 512], BF16), ("sgl", [128, 512], BF16),
                                  ("Sbf", [128, 2, 128], BF16), ("Gk", [128, 32], F32), ("Gq", [128, 32], F32), ("D32", [128, 16], F32)])
        self.a_m4 = layout(pre + [("merged", [128, 16, T], BF16), ("sgt", [128, 2, 512], BF16), ("tm2", [128, 2, 512], F32)])
        self.a_xa = layout([("kmT", [128, 4, MEM], BF16), ("vm", [128, 2, 512], BF16), ("xq", [128, 4, T], BF16),
                            ("xp", [128, 2, 512], BF16), ("xoT", [128, 4, T], BF16), ("xrd", [128, 2, 512], F32)])
        self.a_ff = layout([("actT", [128, 44, T], BF16), ("fsg", [128, 2, 512], F32)])
        self.a_fo = layout([("otok", [128, 4, D], F32), ("ynf", [128, 2, 512], F32)])

    def setup(self):
        dr = self.dr
        a0 = self.a0
        cf = a0["cf"]
        self.dma(self.sp, [(cf[:, :], dr["consts"][:, :])], "par", writes=[cf[:, :]])
        self.dma(self.sp, [(self.pcols[:, :], dr["pcols"][:, :])], "par", writes=[self.pcols[:, :]])
        self.v_copy(self.ident_f[:, :], cf[:, C_ID:C_ID + 128])
        self.v_copy(self.ident_b[:, :], cf[:, C_ID:C_ID + 128])
        self.v_memset(self.ones_b[:, :], 1.0)
        self.v_memset(self.epsc[:, :], EPS)
        self.v_memset(self.onec[:, :], 1.0)
        self.v_copy(self.onesEO[:, 0, :], cf[:, C_OE:C_OE + 128])
        self.v_copy(self.onesEO[:, 1, :], cf[:, C_OO:C_OO + 128])
        self.v_copy(self.maskBD[:, :], cf[:, C_BD:C_BD + 128])
        self.v_copy(self.maskADJ[:, :], cf[:, C_ADJ:C_ADJ + 128])
        self.v_copy(self.mcur01[:, :], cf[:, C_CUR:C_CUR + 128])
        self.v_copy(self.maskrow[:, 0, :], cf[:, C_MR:C_MR + 512])
        self.v_copy(self.maskrow[:, 1, :], cf[:, C_MR0:C_MR0 + 512])
        self.v_copy(self.reset_f[:, :], cf[:, C_RST:C_RST + 512])
        hl = self.pcols[:, PC_LB:PC_LB + 32].rearrange("p (l h) -> p l h", l=NL)
        e, ssum, sm = a0["hl_e"], a0["hl_s"], a0["hl_sm"]
        self.a_act(e[:, :, :], hl, AF.Exp)
        self.v_tt(ssum[:, :], e[:, 0, :], e[:, 1, :], ALU.add)
        self.v_tt(ssum[:, :], ssum[:, :], e[:, 2, :], ALU.add)
        self.v_tt(ssum[:, :], ssum[:, :], e[:, 3, :], ALU.add)
        self.v_recip(ssum[:, :], ssum[:, :])
        for l in range(NL):
            self.v_tt(sm[:, l, :], e[:, l, :], ssum[:, :], ALU.mult)
        self.v_memset(self.oml[:, 0, :], 1.0)
        for l in range(1, NL):
            self.v_tt(self.oml[:, l, :], self.oml[:, l - 1, :], sm[:, l, :], ALU.subtract)
        self.v_ts(self.lbm1[:, :, :], self.oml[:, :, :], -1.0, None, ALU.mult)
        for l in range(NL):
            self.a_act(self.esink[:, l, :], self.pcols[:, l * PL + 49:l * PL + 57], AF.Exp)

    def gcol(self, l, which):
        base = l * PL + 16 * which
        return self.pcols[:, base:base + 16]

    def mem_prep(self, s):
        am = self.a_m
        mt, junk, mss, mrs = am["memtok"], am["mjunk"], am["mss"], am["mrs"]
        src = self.dr["mem"][s].rearrange("(b p) d -> p b d", p=128)
        self.dma(self.sp, [(mt[:, :, :], src)], "mld", writes=[mt[:, :, :]])
        for b in range(2):
            self.a_act(junk[:, :], mt[:, b, :], AF.Square, accum=mss[:, b:b + 1])
        self.a_act(mrs[:, :], mss[:, :], AF.Ln, scale=1.0 / D, bias=self.epsc[:, 0:1])
        self.a_act(mrs[:, :], mrs[:, :], AF.Exp, scale=-0.5)
        for b in range(2):
            self.v_ts(mt[:, b, :], mt[:, b, :], mrs[:, b:b + 1], None, ALU.mult)
        for c in range(16):
            ps = self.ps[c % 2]
            for b in range(2):
                self.tr(ps[:, b * 128:(b + 1) * 128], mt[:, b, c * 128:(c + 1) * 128], self.ident_f[:, :])
            self.a_act(self.memnT[:, c, :], ps[:, 0:256], AF.Copy, scale=self.pcols[:, PC_MEMN + c:PC_MEMN + c + 1])

    def reset_carry(self, nl):
        for l in range(nl):
            self.v_memset(self.Sst[l][:, :, :], 0.0)
            self.v_memset(self.kprev[l][:, :, :, :], 0.0)
            self.v_memset(self.vprev[l][:, :, :, :], 0.0)

    def load_x(self, s, ti):
        xt = self.a_x["xtok"]
        src = self.dr["x"][s, ti * T:(ti + 1) * T, :].rearrange("(b p) d -> p b d", p=128)
        self.dma(self.sp, [(xt[:, :, :], src)], "xld", writes=[xt[:, :, :]])
        for c in range(16):
            ps = self.ps[c % 2]
            for b in range(4):
                self.tr(ps[:, b * 128:(b + 1) * 128], xt[:, b, c * 128:(c + 1) * 128], self.ident_f[:, :])
            if c % 2 == 0:
                self.a_act(self.xT[:, c, :], ps[:, :], AF.Copy)
            else:
                self.v_copy(self.xT[:, c, :], ps[:, :])

    def rmsnorm(self, gcols):
        pss = self.ps[7]
        for c in range(16):
            sq = self.sqb[:, c % 2, :]
            self.a_act(sq, self.xT[:, c, :], AF.Square)
            self.mm(pss[:, :], [(self.ones_b[:, :], sq)], start=(c == 0))
        self.a_act(self.rstd[:, :], pss[:, :], AF.Ln, scale=1.0 / D, bias=self.epsc[:, 0:1])
        self.a_act(self.rstd[:, :], self.rstd[:, :], AF.Exp, scale=-0.5)
        for c in range(16):
            self.v_stt(self.hT[:, c, :], self.xT[:, c, :], gcols[:, c:c + 1], self.rstd[:, :], ALU.mult, ALU.mult)

    def proj_fm(self, W, m, rhs_of_k, kc, ps, n=T):
        self.mm(ps, [(W[:, k, m * 128:(m + 1) * 128], rhs_of_k(k)) for k in range(kc)])

    def proj_tm(self, W, lhs_of_k, kc, ps, c0, c1):
        self.mm(ps, [(lhs_of_k(k), W[:, k, c0:c1]) for k in range(kc)])

    def gelu(self, out, ps, tA, tB):
        self.a_act(tA, ps, AF.Square)
        self.v_ts(tA, tA, 0.044715, 1.0, ALU.mult, ALU.add)
        self.v_tt(tB, ps, tA, ALU.mult)
        self.a_act(tB, tB, AF.Sigmoid, scale=1.5957691216057308)
        self.v_tt(out, ps, tB, ALU.mult)

    def mixer(self, l, first_tile):
        hT = self.hT
        self.rmsnorm(self.gcol(l, 0))
        hk = lambda k: hT[:, k, :]
        dr = self.dr
        self.dma(self.sp, [(self.sg_gb[:, 0, :], dr["sgu_g"][l].partition_broadcast(128)),
                           (self.sg_gb[:, 1, :], dr["sgu_bt"][l].partition_broadcast(128)),
                           (self.bsb[:, :, :].rearrange("p g t -> p (g t)"), dr["sgu_bs"][l].partition_broadcast(128))],
                 "par", writes=[self.sg_gb[:, :, :], self.bsb[:, :, :]])
        a1 = self.a_m1
        yaT, qT, kTpad, vaug, pT, rden = a1["yaT"], a1["qT"], a1["kTpad"], a1["vaug"], a1["pT"], a1["rden"]
        self.v_memset(kTpad[:, :, :, :], 0.0)
        self.v_memset(vaug[:, :, :, :, :], 0.0)
        ev = 0
        for j in range(4):
            W = self.wget(l, "q%d" % j)
            for m in range(2):
                c = 2 * j + m
                ps = self.ps[ev % 2]
                ev += 1
                self.proj_fm(W, m, hk, 16, ps[:, :])
                if c % 2 == 0:
                    self.a_act(qT[:, c, :], ps[:, :], AF.Copy)
                else:
                    self.v_copy(qT[:, c, :], ps[:, :])
        for b in range(2):
            W = self.wget(l, "kd%d" % b)
            for m in range(2):
                g = 2 * b + m
                ps = self.ps[ev % 2]
                ev += 1
                self.proj_fm(W, m, hk, 16, ps[:, :])
                self.a_act(kTpad[0:64, g, 0, :], ps[0:64, :], AF.Copy)
                self.v_copy(kTpad[64:128, g, 1, :], ps[64:128, :])
        W = self.wget(l, "vv")
        for blk in range(4):
            ps = self.ps[ev % 2]
            ev += 1
            self.proj_tm(W, lambda k: hT[:, k, blk * 128:(blk + 1) * 128], 16, ps[:, 0:256], 0, 256)
            src = ps[:, 0:256].rearrange("p (g e) -> p g e", g=4)
            self.a_act(vaug[:, blk, :, 0, 0:64], src, AF.Copy)
            self.v_copy(vaug[:, blk, :, 1, 64:128], src)
        kp, vp = self.kprev[l], self.vprev[l]
        for c in range(8):
            g = c // 2
            ops = self.ps[4 + (c % 2)]
            dps = self.ps[6]
            for blk in range(4):
                sc = self.ps[blk]
                first_block = first_tile and blk == 0
                for kb in range(2):
                    for e in range(2):
                        j = kb * 2 + e
                        if kb == 0:
                            kk = kp[:, g, e, :] if blk == 0 else kTpad[:, g, e, (blk - 1) * 128:blk * 128]
                        else:
                            kk = kTpad[:, g, e, blk * 128:(blk + 1) * 128]
                        self.mm(sc[:, j * 128:(j + 1) * 128], [(kk, qT[:, c, blk * 128:(blk + 1) * 128])], start=(j == 0))
                self.mm(sc[:, :], [(self.ident_b[:, :], self.maskrow[:, 1 if first_block else 0, :])], start=False)
                self.a_act(pT[:, blk, :], sc[:, :], AF.Exp, scale=0.125)
                pairs_o = []
                pairs_d = []
                for kb in range(2):
                    for e in range(2):
                        j = kb * 2 + e
                        if kb == 0:
                            vv = vp[:, g, e, :] if blk == 0 else vaug[:, blk - 1, g, e, :]
                        else:
                            vv = vaug[:, blk, g, e, :]
                        pairs_o.append((vv, pT[:, blk, j * 128:(j + 1) * 128]))
                        pairs_d.append((self.onesEO[:, e, :], pT[:, blk, j * 128:(j + 1) * 128]))
                self.mm(ops[:, blk * 128:(blk + 1) * 128], pairs_o, start=(blk == 0))
                self.mm(dps[:, blk * 128:(blk + 1) * 128], pairs_d, start=(blk == 0))
            rd = rden[:, c % 2, :]
            self.a_act(rd, dps[:, :], AF.Ln, bias=self.esink[:, l, c:c + 1])
            self.a_act(rd, rd, AF.Exp, scale=-1.0)
            self.v_tt(yaT[:, c, :], ops[:, :], rd, ALU.mult)
        self.v_copy(kp[:, :, :, :], kTpad[:, :, :, 384:512])
        self.a_act(vp[:, :, :, :], vaug[:, 3, :, :, :], AF.Copy)

        a2 = self.a_m2
        ybT, uT, vn, vg, gA, gB, bnst, mv, lrs = (a2[k] for k in ("ybT", "uT", "vn", "vg", "gA", "gB", "bnst", "mv", "lrs"))
        self.dma(self.sp, [(vg[:, 0, :], dr["sgu_wT"][l])], "par2", writes=[vg[:, 0, :]])
        self.v_tt(self.WsT[:, :, :], vg[:, 0, :].rearrange("p (g t) -> p g t", g=8),
                  self.mcur01[:, :].unsqueeze(1).broadcast_to([128, 8, 128]), ALU.mult)
        ev = 0
        for j in range(4):
            W = self.wget(l, "u%d" % j)
            for m in range(2):
                c = 2 * j + m
                ps = self.ps[ev % 2]
                self.proj_fm(W, m, hk, 16, ps[:, :])
                self.gelu(uT[:, c, :], ps[:, :], gA[:, ev % 2, :], gB[:, ev % 2, :])
                ev += 1
        Wv = [self.wget(l, "vb%d" % j) for j in range(4)]
        for blk in range(4):
            vgb = vg[:, blk % 2, :]
            for j in range(4):
                ps = self.ps[2 + (ev % 2)]
                self.proj_tm(Wv[j], lambda k: hT[:, k, blk * 128:(blk + 1) * 128], 16, ps[:, 0:256], 0, 256)
                self.gelu(vgb[:, j * 256:(j + 1) * 256], ps[:, 0:256], gA[:, ev % 2, 0:256], gB[:, ev % 2, 0:256])
                ev += 1
            bs = bnst[:, blk % 2, :, :]
            for hh in range(2):
                self.v_bnstats(bs[:, hh, :], vgb[:, hh * 512:(hh + 1) * 512])
            mvb = mv[:, blk % 2, :]
            self.v_bnaggr(mvb, bs.rearrange("p a b -> p (a b)"))
            lr = lrs[:, blk % 2:blk % 2 + 1]
            self.a_act(lr, mvb[:, 1:2], AF.Ln, bias=self.epsc[:, 0:1])
            self.a_act(lr, lr, AF.Exp, scale=-0.5)
            self.v_ts(vgb, vgb, mvb[:, 0:1], lr, ALU.subtract, ALU.mult)
            self.v_tt(vgb, vgb, self.sg_gb[:, 0, :], ALU.mult)
            self.v_tt(vn[:, blk, :], vgb, self.sg_gb[:, 1, :], ALU.add)
        for g in range(8):
            ps = self.ps[4 + (g % 2)]
            for blk in range(4):
                self.mm(ps[:, blk * 128:(blk + 1) * 128], [(vn[:, blk, g * 128:(g + 1) * 128], self.WsT[:, g, :])], start=(blk == 0))
            tmp = gA[:, g % 2, :]
            self.v_tt(tmp.rearrange("p (b t) -> p b t", b=4), ps[:, :].rearrange("p (b t) -> p b t", b=4),
                      self.bsb[:, g, :].unsqueeze(1).broadcast_to([128, 4, 128]), ALU.add)
            self.v_tt(ybT[:, g, :], tmp, uT[:, g, :], ALU.mult)

        a3 = self.a_m3
        ycT, vtok, hA, hB, hC, sq, q16, q32, k16, kf16, kf32, ketok, a1m, a2m, sgl, Sbf, Gk, Gq, D32 = (a3[k] for k in (
            "ycT", "vtok", "hA", "hB", "hC", "sq", "q16", "q32", "k16", "kf16", "kf32", "ketok", "a1m", "a2m", "sgl", "Sbf", "Gk", "Gq", "D32"))
        self.v_memset(ketok[:, :, :, :], 0.0)
        Wi = [self.wget(l, "ic%d" % j) for j in range(4)]
        ev = 0
        for blk in range(4):
            for j in range(4):
                ps = self.ps[ev % 2]
                self.proj_tm(Wi[j], lambda k: hT[:, k, blk * 128:(blk + 1) * 128], 16, ps[:, 0:256], 0, 256)
                if ev % 2 == 0:
                    self.a_act(vtok[:, blk, j * 256:(j + 1) * 256], ps[:, 0:256], AF.Copy)
                else:
                    self.v_copy(vtok[:, blk, j * 256:(j + 1) * 256], ps[:, 0:256])
                ev += 1
        S = self.Sst[l]
        gn = self.pcols[:, l * PL + 48:l * PL + 49]
        sstep = 0
        v3 = lambda t: t[:, :].rearrange("p (s i) -> p s i", i=16)
        b4 = lambda t: t[:, :].rearrange("p (b t) -> p b t", b=4)
        for hp in range(4):
            Wf = self.wget(l, "fc%d" % hp)
            Wq = self.wget(l, "qc%d" % hp)
            Wg = self.wget(l, "gc%d" % hp)
            for m in range(2):
                h = 2 * hp + m
                hsl = slice(h * 128, (h + 1) * 128)
                psf, psq, psg = self.ps[0], self.ps[1], self.ps[2]
                self.proj_fm(Wf, m, hk, 16, psf[:, :])
                self.proj_fm(Wq, m, hk, 16, psq[:, :])
                self.proj_fm(Wg, m, hk, 16, psg[:, :])
                self.a_act(hA[:, :], psf[:, :], AF.Sigmoid, scale=-1.0)
                self.a_act(hB[:, :], hA[:, :], AF.Ln, scale=self.lbm1[:, l, h:h + 1], bias=self.onec[:, 0:1])
                self.v_scan(hC[:, :], self.reset_f[:, :], hB[:, :], 0.0, ALU.mult, ALU.add)
                self.a_act(hB[:, :], hC[:, :], AF.Exp)
                self.a_act(hC[:, :], hC[:, :], AF.Exp, scale=-1.0)
                self.a_act(sq[:, :], psq[:, :], AF.Silu)
                self.v_stt(q16[:, :], sq[:, :], 128.0 ** -0.5, hB[:, :], ALU.mult, ALU.mult)
                self.v_stt(k16[:, :], hA[:, :], self.oml[:, l, h:h + 1], hC[:, :], ALU.mult, ALU.mult)
                E = v3(hB)[:, :, 15]
                E2 = hB[:, :].rearrange("p (c j i) -> p c j i", j=2, i=16)
                self.v_tt(v3(kf16), v3(k16), E.unsqueeze(2).broadcast_to([128, 32, 16]), ALU.mult)
                self.v_memset(Gk[:, :], 1.0)
                self.v_memset(Gq[:, :], 1.0)
                Gk2 = Gk[:, :].rearrange("p (c j) -> p c j", j=2)
                Gq2 = Gq[:, :].rearrange("p (c j) -> p c j", j=2)
                self.v_copy(Gk2[:, :, 0], E2[:, :, 1, 15])
                self.v_copy(Gq2[:, :, 1], E2[:, :, 0, 15])
                self.v_tt(D32[:, :], E2[:, :, 0, 15], E2[:, :, 1, 15], ALU.mult)
                self.v_tt(v3(kf32), v3(kf16), Gk[:, :].unsqueeze(2).broadcast_to([128, 32, 16]), ALU.mult)
                self.v_tt(v3(q32), v3(q16), Gq[:, :].unsqueeze(2).broadcast_to([128, 32, 16]), ALU.mult)
                self.a_act(sgl[:, :], psg[:, :], AF.Silu)
                pk = self.ps[3]
                for blk in range(4):
                    self.mm(pk[:, blk * 128:(blk + 1) * 128], [(kf32[:, blk * 128:(blk + 1) * 128], self.ident_b[:, :])], start=(blk == 0))
                pk3 = b4(pk)
                for v in range(4):
                    rs = slice(32 * v, 32 * v + 32)
                    if v % 2 == 0:
                        self.a_act(ketok[rs, v, :, :], pk3[rs, :, :], AF.Copy)
                    else:
                        self.v_copy(ketok[rs, v, :, :], pk3[rs, :, :])
                pa, pa2 = self.ps[4], self.ps[7]
                for blk in range(4):
                    sl = slice(blk * 128, (blk + 1) * 128)
                    self.mm(pa[:, sl], [(k16[:, sl], q16[:, sl])], start=(blk == 0))
                self.v_tt(b4(a1m), b4(pa), self.maskBD[:, :].unsqueeze(1).broadcast_to([128, 4, 128]), ALU.mult)
                for blk in range(4):
                    sl = slice(blk * 128, (blk + 1) * 128)
                    self.mm(pa2[:, sl], [(kf16[:, sl], q16[:, sl])], start=(blk == 0))
                self.v_tt(b4(a2m), b4(pa2), self.maskADJ[:, :].unsqueeze(1).broadcast_to([128, 4, 128]), ALU.mult)
                po = self.ps[5]
                self.a_act(Sbf[:, sstep % 2, :], S[:, h, :], AF.Copy)
                for cch in range(16):
                    blk, v = cch // 4, cch % 4
                    sl = slice(blk * 128, (blk + 1) * 128)
                    if v == 0:
                        self.mm(po[:, sl], [(vtok[:, blk, hsl], a1m[:, sl]), (vtok[:, blk, hsl], a2m[:, sl])], start=(cch == 0))
                    cs = slice(cch * 32, (cch + 1) * 32)
                    self.mm(po[:, cs], [(Sbf[:, sstep % 2, :], q32[:, cs])], start=False)
                    pst = self.ps[6]
                    pss = pst[:, (cch % 4) * 128:(cch % 4 + 1) * 128]
                    self.mm(pss, [(ketok[:, v, blk, :], vtok[:, blk, hsl])], start=True)
                    self.v_stt(S[:, h, :], S[:, h, :], D32[:, cch:cch + 1], pss, ALU.mult, ALU.add)
                    sstep += 1
                    if cch < 15:
                        self.a_act(Sbf[:, sstep % 2, :], S[:, h, :], AF.Copy)
                osq = self.sqb[:, 0, :]
                self.a_act(osq, po[:, :], AF.Square)
                pn = self.ps[7]
                self.mm(pn[:, :], [(self.ones_b[:, :], osq)])
                self.a_act(self.rstd[:, :], pn[:, :], AF.Ln, scale=1.0 / 128, bias=self.epsc[:, 0:1])
                self.a_act(self.rstd[:, :], self.rstd[:, :], AF.Exp, scale=-0.5)
                self.v_stt(hA[:, :], po[:, :], gn, self.rstd[:, :], ALU.mult, ALU.mult)
                self.v_tt(ycT[:, h, :], hA[:, :], sgl[:, :], ALU.mult)

        a4 = self.a_m4
        merged, sgt, tm2 = a4["merged"], a4["sgt"], a4["tm2"]
        ys = [a4["yaT"], a4["ybT"], a4["ycT"]]
        ev = 0
        for i in range(3):
            for j in range(8):
                Wg = self.wget(l, "gt%d_%d" % (i, j))
                Wb = self.wget(l, "br%d_%d" % (i, j))
                for m in range(2):
                    n = 2 * j + m
                    pg = self.ps[ev % 2]
                    pb = self.ps[2 + ev % 2]
                    self.proj_fm(Wg, m, hk, 16, pg[:, :])
                    self.proj_fm(Wb, m, lambda k: ys[i][:, k, :], 8, pb[:, :])
                    self.a_act(sgt[:, ev % 2, :], pg[:, :], AF.Sigmoid)
                    if i == 0:
                        self.v_tt(merged[:, n, :], pb[:, :], sgt[:, ev % 2, :], ALU.mult)
                    else:
                        self.v_tt(tm2[:, ev % 2, :], pb[:, :], sgt[:, ev % 2, :], ALU.mult)
                        self.v_tt(merged[:, n, :], merged[:, n, :], tm2[:, ev % 2, :], ALU.add)
                    ev += 1
        for j in range(8):
            W = self.wget(l, "out%d" % j)
            for m in range(2):
                n = 2 * j + m
                ps = self.ps[4 + ev % 2]
                ev += 1
                self.proj_fm(W, m, lambda k: merged[:, k, :], 16, ps[:, :])
                self.v_tt(self.xT[:, n, :], self.xT[:, n, :], ps[:, :], ALU.add)

    def xattn(self, l):
        hT = self.hT
        self.rmsnorm(self.gcol(l, 1))
        ax = self.a_xa
        kmT, vm, xq, xp, xoT, xrd = (ax[k] for k in ("kmT", "vm", "xq", "xp", "xoT", "xrd"))
        mk = lambda k: self.memnT[:, k, :]
        ev = 0
        for j in range(2):
            W = self.wget(l, "xk%d" % j)
            for m in range(2):
                ps = self.ps[ev % 2]
                ev += 1
                self.proj_fm(W, m, mk, 16, ps[:, 0:MEM], n=MEM)
                self.a_act(kmT[:, 2 * j + m, :], ps[:, 0:MEM], AF.Copy)
        for j in range(2):
            W = self.wget(l, "xv%d" % j)
            for mb in range(2):
                ps = self.ps[ev % 2]
                ev += 1
                self.proj_tm(W, lambda k: self.memnT[:, k, mb * 128:(mb + 1) * 128], 16, ps[:, 0:256], 0, 256)
                self.v_copy(vm[:, mb, j * 256:(j + 1) * 256], ps[:, 0:256])
        for j in range(2):
            W = self.wget(l, "xq%d" % j)
            for m in range(2):
                ps = self.ps[ev % 2]
                ev += 1
                self.proj_fm(W, m, lambda k: hT[:, k, :], 16, ps[:, :])
                self.a_act(xq[:, 2 * j + m, :], ps[:, :], AF.Copy)
        for hd in range(4):
            po = self.ps[4 + hd % 2]
            pd = self.ps[6 + hd % 2]
            for mb in range(2):
                sc = self.ps[2 + mb]
                self.mm(sc[:, :], [(kmT[:, hd, mb * 128:(mb + 1) * 128], xq[:, hd, :])])
                self.a_act(xp[:, mb, :], sc[:, :], AF.Exp, scale=128.0 ** -0.5)
            self.mm(po[:, :], [(vm[:, mb, hd * 128:(hd + 1) * 128], xp[:, mb, :]) for mb in range(2)])
            self.mm(pd[:, :], [(self.ones_b[:, :], xp[:, mb, :]) for mb in range(2)])
            rd = xrd[:, hd % 2, :]
            self.a_act(rd, pd[:, :], AF.Ln)
            self.a_act(rd, rd, AF.Exp, scale=-1.0)
            self.v_tt(xoT[:, hd, :], po[:, :], rd, ALU.mult)
        ev = 0
        for j in range(2):
            W = self.wget(l, "xo%d" % j)
            for m in range(8):
                n = 8 * j + m
                ps = self.ps[ev % 2]
                ev += 1
                self.proj_fm(W, m, lambda k: xoT[:, k, :], 4, ps[:, :])
                self.v_tt(self.xT[:, n, :], self.xT[:, n, :], ps[:, :], ALU.add)

    def ffn(self, l):
        hT = self.hT
        self.rmsnorm(self.gcol(l, 2))
        af = self.a_ff
        actT, fsg = af["actT"], af["fsg"]
        hk = lambda k: hT[:, k, :]
        ev = 0
        for j in range(22):
            Wg = self.wget(l, "fg%d" % j)
            Wu = self.wget(l, "fu%d" % j)
            for m in range(2):
                f = 2 * j + m
                pg = self.ps[ev % 2]
                pu = self.ps[2 + ev % 2]
                self.proj_fm(Wg, m, hk, 16, pg[:, :])
                self.proj_fm(Wu, m, hk, 16, pu[:, :])
                self.a_act(fsg[:, ev % 2, :], pg[:, :], AF.Silu)
                self.v_tt(actT[:, f, :], pu[:, :], fsg[:, ev % 2, :], ALU.mult)
                ev += 1
        for npair in range(8):
            pss = [self.ps[4 + 2 * (npair % 2)], self.ps[5 + 2 * (npair % 2)]]
            for q in range(4):
                W = self.wget(l, "fo%d_%d" % (npair, q))
                for m in range(2):
                    self.mm(pss[m][:, :], [(W[:, k, m * 128:(m + 1) * 128], actT[:, q * 11 + k, :]) for k in range(11)], start=(q == 0))
            for m in range(2):
                n = 2 * npair + m
                self.v_tt(self.xT[:, n, :], self.xT[:, n, :], pss[m][:, :], ALU.add)

    def final_out(self, s, ti):
        fo = self.a_fo
        otok, ynf = fo["otok"], fo["ynf"]
        pss = self.ps[7]
        for c in range(16):
            sq = self.sqb[:, c % 2, :]
            self.a_act(sq, self.xT[:, c, :], AF.Square)
            self.mm(pss[:, :], [(self.ones_b[:, :], sq)], start=(c == 0))
        self.a_act(self.rstd[:, :], pss[:, :], AF.Ln, scale=1.0 / D, bias=self.epsc[:, 0:1])
        self.a_act(self.rstd[:, :], self.rstd[:, :], AF.Exp, scale=-0.5)
        for c in range(16):
            yn = ynf[:, c % 2, :]
            self.v_stt(yn, self.xT[:, c, :], self.pcols[:, PC_FINAL + c:PC_FINAL + c + 1], self.rstd[:, :], ALU.mult, ALU.mult)
            ps = self.ps[c % 2]
            for b in range(4):
                self.tr(ps[:, b * 128:(b + 1) * 128], yn[:, b * 128:(b + 1) * 128], self.ident_f[:, :])
            dst = otok[:, :, c * 128:(c + 1) * 128]
            src = ps[:, :].rearrange("p (b f) -> p b f", b=4)
            if c % 2 == 0:
                self.a_act(dst, src, AF.Copy)
            else:
                self.v_copy(dst, src)
        dsty = self.y[s, ti * T:(ti + 1) * T, :].rearrange("(b p) d -> p b d", p=128)
        self.dma(self.sp, [(dsty, otok[:, :, :])], "st0", reads=[otok[:, :, :]])

    def replay(self):
        nc = self.nc
        import contextlib
        with contextlib.ExitStack() as st:
            sems = {}
            for k in self.semkeys:
                sems[k] = st.enter_context(nc.semaphore(k))
            block = st.enter_context(nc.Block())

            def run(E, h):
                for waits, fn, ev in E.prog:
                    for k, v in waits:
                        h.wait_ge(sems[k], v)
                    if fn is None:
                        continue
                    ins = fn(h)
                    if isinstance(ins, list):
                        for i in ins:
                            i.then_inc(sems[ev[0]], 16)
                    else:
                        ins.then_inc(sems[ev[0]], 1)

            @block.tensor
            def _(h):
                run(self.pe, h)

            @block.scalar
            def _(h):
                run(self.act, h)

            @block.vector
            def _(h):
                run(self.dve, h)

            @block.sync
            def _(h):
                run(self.sp, h)

            @block.gpsimd
            def _(h):
                run(self.pool, h)


def _make_consts():
    c = np.zeros((128, NCC), np.float32)
    s = np.arange(128)[:, None]
    t = np.arange(128)[None, :]
    c[:, C_ID:C_ID + 128] = (s == t)
    c[:, C_BD:C_BD + 128] = ((s // 16 == t // 16) & (s <= t))
    c[:, C_ADJ:C_ADJ + 128] = ((s // 32 == t // 32) & (s % 32 < 16) & (t % 32 >= 16))
    c[:, C_CUR:C_CUR + 128] = (s <= t)
    cur = np.where(s <= t, 0.0, NEG)
    prev = np.where(s > t, 0.0, NEG)
    neg = np.full((128, 128), NEG)
    c[:, C_MR:C_MR + 512] = np.concatenate([prev, prev, cur, cur], axis=1)
    c[:, C_MR0:C_MR0 + 512] = np.concatenate([neg, neg, cur, cur], axis=1)
    tt = np.arange(512)[None, :]
    c[:, C_RST:C_RST + 512] = np.broadcast_to((tt % 16 != 0), (128, 512))
    c[:, C_OE:C_OE + 128] = np.broadcast_to(t < 64, (128, 128))
    c[:, C_OO:C_OO + 128] = np.broadcast_to(t >= 64, (128, 128))
    return c


def _make_pcols(inp):
    pc = np.zeros((128, NPCOL), np.float32)

    def col16(v):
        return np.ascontiguousarray(np.asarray(v, np.float32).reshape(16, 128).T)

    for l in range(NL):
        b = l * PL
        pc[:, b:b + 16] = col16(inp["norm_mix"][l])
        pc[:, b + 16:b + 32] = col16(inp["norm_x"][l])
        pc[:, b + 32:b + 48] = col16(inp["norm_ffn"][l])
        pc[:, b + 48] = np.asarray(inp["hgrn_norm"][l], np.float32)
        sk = np.asarray(inp["sinks"][l], np.float32)
        for c in range(8):
            pc[0:64, b + 49 + c] = sk[2 * c]
            pc[64:128, b + 49 + c] = sk[2 * c + 1]
        pc[:, PC_LB + l * 8:PC_LB + l * 8 + 8] = np.asarray(inp["hgrn_lb"][l], np.float32).reshape(8, 128).T
    pc[:, PC_FINAL:PC_FINAL + 16] = col16(inp["final_norm"])
    pc[:, PC_MEMN:PC_MEMN + 16] = col16(inp["mem_norm"])
    return pc


def kernel(**inputs):
    cfg = {"nseq": 2, "ntile": 4, "nl": NL}
    return run_cfg(cfg, inputs)


def run_cfg(cfg, inputs, trace=False):
    ncores = 8
    f = lambda k: np.ascontiguousarray(np.asarray(inputs[k], dtype=np.float32))
    x = f("x")
    mem = f("mem")
    shared = {k: f(k) for k in ["w_in", "w_gate", "w_br_a", "w_br_b", "w_br_c", "w_out", "w_xq", "w_xkv", "w_xo", "w_ffn_in", "w_ffn_out"]}
    shared["pcols"] = _make_pcols(inputs)
    shared["consts"] = _make_consts()
    shared["sgu_g"] = f("sgu_ln_g")
    shared["sgu_bt"] = f("sgu_ln_b")
    shared["sgu_bs"] = np.ascontiguousarray(f("sgu_b").reshape(NL, 1024))
    shared["sgu_wT"] = np.ascontiguousarray(f("sgu_w").transpose(0, 3, 1, 2).reshape(NL, 128, 1024))
    ns = cfg["nseq"]
    p = Prog(cfg)
    nc = p.build()
    in_maps = []
    for c in range(ncores):
        m = dict(shared)
        m["x"] = np.ascontiguousarray(x[ns * c:ns * c + ns]) if ns == 2 else np.ascontiguousarray(x[2 * c:2 * c + 1])
        m["mem"] = np.ascontiguousarray(mem[ns * c:ns * c + ns]) if ns == 2 else np.ascontiguousarray(mem[2 * c:2 * c + 1])
        in_maps.append(m)
    res = run_bass_kernel_spmd(nc, in_maps, core_ids=list(range(ncores)), trace=trace)
    outs = [np.asarray(r["y"]) for r in res.results]
    if cfg.get("dbg"):
        run_cfg.last_dbg = [np.asarray(r["dbg"]) for r in res.results]
    run_cfg.last_res = res
    return np.concatenate(outs, axis=0).astype(np.float32)
```
